# Optimizing a Trainium2 kernel written in Bass

```python
import math
import jax, jax.numpy as jnp
from jax import lax
import numpy as np

D_MODEL = 2048
BATCH = 1
SEQ = 16384
DEPTH = 2

DIFF_HEADS = 8
DIFF_QK_DIM = 64
DIFF_V_DIM = 2 * DIFF_QK_DIM
DIFF_DIM = DIFF_HEADS * DIFF_V_DIM
Q_BLOCK = 128
RWKV_HEADS = 16
RWKV_HEAD_DIM = 64
RWKV_DIM = RWKV_HEADS * RWKV_HEAD_DIM
DECAY_LORA = 64
AAA_LORA = 64
GATE_LORA = 160
RWKV_IN = 3 * RWKV_DIM + 2 * DECAY_LORA + 2 * AAA_LORA + GATE_LORA
N_MEM = 256
MEM_HEADS = 4
MEM_HEAD_DIM = 256
MEM_DIM = MEM_HEADS * MEM_HEAD_DIM
N_BRANCH = 3
BRANCH_DIM = 1024
D_FF = 5632
CONV_WIDTH = 3
N_IN = 3 * DIFF_DIM + RWKV_IN + MEM_DIM + N_BRANCH * D_MODEL
NORM_EPS = 1e-6
LNX_EPS = 64e-5

kernel_name = "hybrid_diffattn_rwkv7_memxattn_convglu_encoder"


def _rms(x, g):
    xf = x.astype(jnp.float32)
    y = xf * lax.rsqrt(jnp.mean(xf * xf, axis=-1, keepdims=True) + NORM_EPS)
    return y.astype(x.dtype) * g


def _shift_prev(x):
    return jnp.pad(x[:, :-1], ((0, 0), (1, 0), (0, 0)))


def _shift_next(x):
    return jnp.pad(x[:, 1:], ((0, 0), (0, 1), (0, 0)))


def _diff_attention(q, k, v, lam):
    B, S, H = q.shape[:3]
    nb = S // Q_BLOCK
    slopes = jnp.exp2(-8.0 * jnp.arange(1, H + 1, dtype=jnp.float32) / H)
    kpos = jnp.arange(S, dtype=jnp.float32)
    qb = jnp.moveaxis(q.reshape(B, nb, Q_BLOCK, H, 2, q.shape[-1]), 1, 0)
    starts = jnp.arange(nb, dtype=jnp.float32) * Q_BLOCK

    def block(args):
        qblk, start = args
        qpos = start + jnp.arange(Q_BLOCK, dtype=jnp.float32)
        bias = -slopes[:, None, None] * jnp.abs(qpos[:, None] - kpos[None, :])
        s = jnp.einsum('bqhmd,bshmd->bhmqs', qblk, k).astype(jnp.float32) + bias[None, :, None]
        p = jax.nn.softmax(s, axis=-1)
        a = p[:, :, 0] - lam * p[:, :, 1]
        return jnp.einsum('bhqs,bshd->bqhd', a.astype(v.dtype), v)

    o = lax.map(block, (qb, starts))
    return jnp.moveaxis(o, 0, 1).reshape(B, S, H, v.shape[-1])


def _diff_branch(p, qk_g, lam_p, subln_g, lam_init):
    B, S, _ = p.shape
    q, k, v = jnp.split(p, 3, axis=-1)
    q = _rms(q.reshape(B, S, DIFF_HEADS, 2, DIFF_QK_DIM), qk_g[0]) * (DIFF_QK_DIM ** -0.5)
    k = _rms(k.reshape(B, S, DIFF_HEADS, 2, DIFF_QK_DIM), qk_g[1])
    v = v.reshape(B, S, DIFF_HEADS, DIFF_V_DIM)
    lp = lam_p.astype(jnp.float32)
    lam = jnp.exp(jnp.sum(lp[0] * lp[1])) - jnp.exp(jnp.sum(lp[2] * lp[3])) + lam_init
    o = _diff_attention(q, k, v, lam)
    o = _rms(o, subln_g) * (1.0 - lam_init)
    return o.reshape(B, S, DIFF_DIM)


def _wkv7(r, w, k, v, kk, b, reverse):
    B, S, H, N = r.shape

    def step(state, inp):
        r_t, w_t, k_t, v_t, kk_t, b_t = inp
        sa = jnp.einsum('bhvk,bhk->bhv', state, kk_t)
        state = (state * w_t[:, :, None, :] - sa[..., None] * b_t[:, :, None, :]
                 + v_t[..., None] * k_t[:, :, None, :])
        return state, jnp.einsum('bhvk,bhk->bhv', state, r_t)

    xs = tuple(jnp.moveaxis(t, 1, 0) for t in (r, w, k, v, kk, b))
    s0 = jnp.zeros((B, H, N, N), jnp.float32)
    _, ys = lax.scan(step, s0, xs, reverse=reverse)
    return jnp.moveaxis(ys, 0, 1)


def _rwkv_branch(p, mu, w0, w2, a0, a2, g2, k_k, k_a, r_k, lnx_g, lnx_b):
    B, S, _ = p.shape
    H, N = RWKV_HEADS, RWKV_HEAD_DIM
    f32 = jnp.float32
    p = p + mu[0] * (_shift_prev(p) - p) + mu[1] * (_shift_next(p) - p)
    c0 = 3 * RWKV_DIM
    r, k, v, lw, la, lg = jnp.split(
        p, [RWKV_DIM, 2 * RWKV_DIM, c0, c0 + 2 * DECAY_LORA, c0 + 2 * DECAY_LORA + 2 * AAA_LORA], axis=-1)
    lw = jnp.tanh(lw.reshape(B, S, 2, DECAY_LORA))
    w_log = -jax.nn.softplus(-(w0 + jnp.einsum('bsdr,drc->bsdc', lw, w2)).astype(f32)) - 0.5
    decay = jnp.exp(-jnp.exp(w_log))
    a = jax.nn.sigmoid((a0 + jnp.einsum('bsdr,drc->bsdc', la.reshape(B, S, 2, AAA_LORA), a2)).astype(f32))
    g = jax.nn.sigmoid(lg) @ g2
    r, k, v = r.astype(f32), k.astype(f32), v.astype(f32)
    kk = (k * k_k).reshape(B, S, H, N)
    kk = kk / jnp.maximum(jnp.sqrt(jnp.sum(kk * kk, axis=-1, keepdims=True)), 1e-12)
    k_dir = k[:, :, None, :] * (1.0 + (a - 1.0) * k_a)
    b_dir = kk.reshape(B, S, 1, RWKV_DIM) * a

    def heads(t):
        return t.reshape(B, S, H, N)

    rh, vh = heads(r), heads(v)
    y = (_wkv7(rh, heads(decay[:, :, 0]), heads(k_dir[:, :, 0]), vh, kk, heads(b_dir[:, :, 0]), reverse=False)
         + _wkv7(rh, heads(decay[:, :, 1]), heads(k_dir[:, :, 1]), vh, kk, heads(b_dir[:, :, 1]), reverse=True))
    mean = jnp.mean(y, axis=-1, keepdims=True)
    var = jnp.mean(jnp.square(y - mean), axis=-1, keepdims=True)
    y = ((y - mean) * lax.rsqrt(var + LNX_EPS)).reshape(B, S, RWKV_DIM) * lnx_g + lnx_b
    bonus = jnp.sum(rh * heads(k) * r_k, axis=-1, keepdims=True) * vh
    y = (y + bonus.reshape(B, S, RWKV_DIM)) * g
    return y.astype(p.dtype)


def _mem_branch(p, mem, mem_norm_g, w_kv, qk_g):
    B, S, _ = p.shape
    M = mem.shape[1]
    q = _rms(p.reshape(B, S, MEM_HEADS, MEM_HEAD_DIM), qk_g[0]) * (MEM_HEAD_DIM ** -0.5)
    kv = (_rms(mem, mem_norm_g) @ w_kv).reshape(B, M, 2, MEM_HEADS, MEM_HEAD_DIM)
    km = _rms(kv[:, :, 0], qk_g[1])
    vm = kv[:, :, 1]
    s = jnp.einsum('bshd,bmhd->bhsm', q, km).astype(jnp.float32)
    pr = jax.nn.softmax(s, axis=-1)
    o = jnp.einsum('bhsm,bmhd->bshd', pr.astype(vm.dtype), vm)
    return o.reshape(B, S, MEM_DIM)


def setup_inputs(seed: int = 0) -> dict:
    key = jax.random.key(seed)
    ks = iter(jax.random.split(key, 32))

    def nrm(shape, scale):
        return jax.random.normal(next(ks), shape, jnp.float32) * scale

    def unif(shape, lo, hi):
        return jax.random.uniform(next(ks), shape, jnp.float32, lo, hi)

    L, D = DEPTH, D_MODEL
    return {
        "x": nrm((BATCH, SEQ, D), 1.0),
        "mem": nrm((BATCH, N_MEM, D), 1.0),
        "attn_norm_g": 1.0 + nrm((L, D), 0.02),
        "w_in": nrm((L, D, N_IN), D ** -0.5),
        "diff_qk_g": 1.0 + nrm((L, 2, 2, DIFF_QK_DIM), 0.02),
        "diff_lambda": nrm((L, 4, DIFF_QK_DIM), 0.1),
        "diff_subln_g": 1.0 + nrm((L, DIFF_V_DIM), 0.02),
        "rwkv_mu": unif((L, 2, RWKV_IN), 0.0, 0.5),
        "rwkv_w0": unif((L, 2, RWKV_DIM), -5.5, 0.5),
        "rwkv_w2": nrm((L, 2, DECAY_LORA, RWKV_DIM), 0.1),
        "rwkv_a0": nrm((L, 2, RWKV_DIM), 0.1),
        "rwkv_a2": nrm((L, 2, AAA_LORA, RWKV_DIM), 0.1),
        "rwkv_g2": nrm((L, GATE_LORA, RWKV_DIM), GATE_LORA ** -0.5),
        "rwkv_k_k": 0.85 + nrm((L, RWKV_DIM), 0.02),
        "rwkv_k_a": 1.0 + nrm((L, RWKV_DIM), 0.02),
        "rwkv_r_k": nrm((L, RWKV_HEADS, RWKV_HEAD_DIM), 0.1),
        "rwkv_lnx_g": 1.0 + nrm((L, RWKV_DIM), 0.02),
        "rwkv_lnx_b": nrm((L, RWKV_DIM), 0.02),
        "mem_norm_g": 1.0 + nrm((L, D), 0.02),
        "w_mem_kv": nrm((L, D, 2 * MEM_DIM), D ** -0.5),
        "mem_qk_g": 1.0 + nrm((L, 2, MEM_HEAD_DIM), 0.02),
        "w_branch": nrm((L, N_BRANCH, BRANCH_DIM, D), BRANCH_DIM ** -0.5),
        "w_out": nrm((L, D, D), D ** -0.5),
        "ffn_norm_g": 1.0 + nrm((L, D), 0.02),
        "w_ffn_up": nrm((L, D, 2 * D_FF), D ** -0.5),
        "ffn_conv_w": nrm((L, CONV_WIDTH, D_FF), CONV_WIDTH ** -0.5),
        "ffn_conv_b": nrm((L, D_FF), 0.02),
        "w_ffn_down": nrm((L, D_FF, D), D_FF ** -0.5),
    }


def reference(x, mem, attn_norm_g, w_in, diff_qk_g, diff_lambda, diff_subln_g,
              rwkv_mu, rwkv_w0, rwkv_w2, rwkv_a0, rwkv_a2, rwkv_g2, rwkv_k_k, rwkv_k_a,
              rwkv_r_k, rwkv_lnx_g, rwkv_lnx_b, mem_norm_g, w_mem_kv, mem_qk_g,
              w_branch, w_out, ffn_norm_g, w_ffn_up, ffn_conv_w, ffn_conv_b, w_ffn_down):
    B, S, _ = x.shape
    splits = [3 * DIFF_DIM, 3 * DIFF_DIM + RWKV_IN, 3 * DIFF_DIM + RWKV_IN + MEM_DIM]
    for l in range(DEPTH):
        lam_init = 0.8 - 0.6 * math.exp(-0.3 * l)
        h = _rms(x, attn_norm_g[l])
        p = h @ w_in[l]
        p_diff, p_rwkv, p_mem, p_gate = jnp.split(p, splits, axis=-1)
        o_diff = _diff_branch(p_diff, diff_qk_g[l], diff_lambda[l], diff_subln_g[l], lam_init)
        o_rwkv = _rwkv_branch(p_rwkv, rwkv_mu[l], rwkv_w0[l], rwkv_w2[l], rwkv_a0[l], rwkv_a2[l],
                              rwkv_g2[l], rwkv_k_k[l], rwkv_k_a[l], rwkv_r_k[l],
                              rwkv_lnx_g[l], rwkv_lnx_b[l])
        o_mem = _mem_branch(p_mem, mem, mem_norm_g[l], w_mem_kv[l], mem_qk_g[l])
        o = jnp.stack([o_diff, o_rwkv, o_mem], axis=2)
        proj = jnp.einsum('bsnc,ncd->bsnd', o, w_branch[l])
        gate = jax.nn.sigmoid(p_gate.reshape(B, S, N_BRANCH, D_MODEL))
        x = x + jnp.sum(gate * proj, axis=2) @ w_out[l]
        h2 = _rms(x, ffn_norm_g[l])
        u_gate, u_val = jnp.split(h2 @ w_ffn_up[l], 2, axis=-1)
        cw = ffn_conv_w[l]
        u_gate = cw[0] * _shift_prev(u_gate) + cw[1] * u_gate + cw[2] * _shift_next(u_gate) + ffn_conv_b[l]
        x = x + (jax.nn.silu(u_gate) * u_val) @ w_ffn_down[l]
    return x
```

```python
import math
import numpy as np
import concourse.bass as bass
import concourse.mybir as mybir
from concourse.bass_utils import run_bass_kernel_spmd

F32 = mybir.dt.float32
BF16 = mybir.dt.bfloat16
AF = mybir.ActivationFunctionType
ALU = mybir.AluOpType
AX = mybir.AxisListType

NCORES = 8


class Tok:
    __slots__ = ("w", "r", "name")

    def __init__(self, name=""):
        self.w = None
        self.r = []
        self.name = name


class _Op:
    __slots__ = ("eng", "fn", "deps", "is_dma", "signal", "sem", "val", "idx", "final")

    def __init__(self, eng, fn, deps, is_dma):
        self.eng = eng
        self.fn = fn
        self.deps = deps
        self.is_dma = is_dma
        self.signal = False
        self.sem = None
        self.val = 0
        self.final = False


class Sched:
    NDMA = 24

    def __init__(self, nc):
        self.nc = nc
        self.ops = []
        self.engs = {"pe": nc.tensor, "act": nc.scalar, "dve": nc.vector, "pool": nc.gpsimd, "sp": nc.sync}

    def tok(self, name=""):
        return Tok(name)

    def toks(self, n, name=""):
        return [Tok(name + str(i)) for i in range(n)]

    def _add(self, eng, fn, reads, writes, is_dma):
        deps = []
        for t in reads:
            if t.w is not None:
                deps.append(t.w)
        for t in writes:
            if t.w is not None:
                deps.append(t.w)
            deps.extend(t.r)
        op = _Op(eng, fn, deps, is_dma)
        op.idx = len(self.ops)
        self.ops.append(op)
        for t in reads:
            t.r.append(op)
        for t in writes:
            t.w = op
            t.r = []
        return op

    def op(self, eng, fn, reads=(), writes=()):
        return self._add(eng, fn, reads, writes, False)

    def dma(self, eng, out, in_, reads=(), writes=(), final=False):
        op = self._add(eng, lambda e: e.dma_start(out=out, in_=in_), reads, writes, True)
        op.final = final
        return op

    def setup(self, stack):
        nc = self.nc
        self.esem = {e: stack.enter_context(nc.semaphore("s_" + e)) for e in self.engs}
        self.dsem = [stack.enter_context(nc.semaphore("d_%d" % i)) for i in range(self.NDMA)]
        self.ecount = {e: 0 for e in self.engs}
        self.dcount = [0] * self.NDMA
        self.dlast = [None] * self.NDMA
        self.dnext = 0
        self.seen = {e: {} for e in self.engs}
        self.finals = []
        self.base = 0
        self.total = 0

    def _wait(self, e, d):
        if d.idx < self.base:
            return
        key = id(d.sem)
        if self.seen[e].get(key, 0) >= d.val:
            return
        self.engs[e].wait_ge(d.sem, d.val)
        self.seen[e][key] = d.val

    def flush(self, last=False):
        ops = self.ops
        if not ops:
            return
        lastop = {}
        for op in ops:
            op.idx += self.total
            for d in op.deps:
                if d.is_dma or d.eng != "pe" or op.eng != "pe" or op.is_dma:
                    d.signal = True
            if op.is_dma:
                op.signal = True
            else:
                lastop[op.eng] = op
        for op in lastop.values():
            op.signal = True
        dmas = []
        for op in ops:
            e = op.eng
            for d in op.deps:
                if (not d.is_dma) and d.eng == "pe" and e == "pe" and not op.is_dma:
                    continue
                self._wait(e, d)
            if op.is_dma:
                slot = self.dnext
                self.dnext = (self.dnext + 1) % self.NDMA
                if self.dlast[slot] is not None:
                    self._wait(e, self.dlast[slot])
                self.dcount[slot] += 16
                op.sem = self.dsem[slot]
                op.val = self.dcount[slot]
                self.dlast[slot] = op
                ins = op.fn(self.engs[e])
                ins.then_inc(op.sem, 16)
            else:
                ins = op.fn(self.engs[e])
                if op.signal:
                    self.ecount[e] += 1
                    op.sem = self.esem[e]
                    op.val = self.ecount[e]
                    ins.then_inc(op.sem, 1)
        for e in self.engs:
            for op in lastop.values():
                self._wait(e, op)
            for d in self.dlast:
                if d is not None:
                    self._wait(e, d)
        self.total += len(ops)
        self.base = self.total
        self.ops = []


class Ctx:
    def __init__(self, nc, stack):
        self.nc = nc
        self.stack = stack
        self.root_stack = stack
        self.S = Sched(nc)
        self._n = 0

    def sb(self, shape, dt, name=None):
        self._n += 1
        return self.stack.enter_context(self.nc.sbuf_tensor("%s_%d" % (name or "sb", self._n), list(shape), dt))

    def ps(self, shape, dt=F32, name=None):
        self._n += 1
        return self.stack.enter_context(self.nc.psum_tensor(name or ("ps%d" % self._n), list(shape), dt))

    def din(self, name, shape, dt=F32):
        return self.nc.dram_tensor(name, list(shape), dt, kind="ExternalInput").ap()

    def dout(self, name, shape, dt=F32):
        return self.nc.dram_tensor(name, list(shape), dt, kind="ExternalOutput").ap()


class RR:
    def __init__(self, items):
        self.items = items
        self.i = 0

    def next(self):
        it = self.items[self.i % len(self.items)]
        self.i += 1
        return it


def const_col(C, val):
    if not hasattr(C, "_consts"):
        C._consts = {}
    if val not in C._consts:
        t = C.sb([128, 1], F32, name="cc%d" % len(C._consts))
        tk = C.S.tok("const")
        C.S.op("pool", lambda e: e.memset(t[:, :], val), writes=[tk])
        C._consts[val] = (t, tk)
    return C._consts[val]


def rmsnorm_fm(C, x_fm, x_tok, g_sb, g_tok, hT_of, hT_tok, nk, ntok, ones_f, ones_tok, ps_sum, ps_tok, tmp_pool, D, eps):
    S = C.S
    epst, epstok = const_col(C, eps)
    for kc in range(nk):
        sq, sqt = tmp_pool.next()
        S.op("act", lambda e, sq=sq, kc=kc: e.activation(out=sq[:, :ntok], in_=x_fm[:, kc, :ntok], func=AF.Square),
             reads=[x_tok], writes=[sqt])
        S.op("pe", lambda e, sq=sq, kc=kc: e.matmul(ps_sum[:, :ntok], lhsT=ones_f[:, :], rhs=sq[:, :ntok],
                                                     start=(kc == 0), stop=(kc == nk - 1)),
             reads=[sqt, ones_tok], writes=[ps_tok])
    rs, rst = tmp_pool.next()
    S.op("act", lambda e: e.activation(out=rs[:, :ntok], in_=ps_sum[:, :ntok], func=AF.Sqrt, scale=1.0 / D,
                                       bias=epst[:, 0:1]),
         reads=[ps_tok, epstok], writes=[rst])
    S.op("dve", lambda e: e.reciprocal(out=rs[:, :ntok], in_=rs[:, :ntok]), reads=[rst], writes=[rst])
    for kc in range(nk):
        S.op("dve", lambda e, kc=kc: e.scalar_tensor_tensor(out=hT_of(kc), in0=x_fm[:, kc, :ntok],
                                                            scalar=g_sb[:, kc:kc + 1], in1=rs[:, :ntok],
                                                            op0=ALU.mult, op1=ALU.mult),
             reads=[x_tok, rst, g_tok], writes=[hT_tok])


class H:
    def __init__(self, C):
        self.C = C
        self.S = C.S

    def act(self, out, in_, func, R, W, **kw):
        self.S.op("act", lambda e: e.activation(out=out, in_=in_, func=func, **kw), R, W)

    def mm(self, out, lhsT, rhs, start, stop, R, W):
        self.S.op("pe", lambda e: e.matmul(out, lhsT=lhsT, rhs=rhs, start=start, stop=stop), R, W)

    def tr(self, out, in_, ident, R, W):
        self.S.op("pe", lambda e: e.transpose(out, in_, ident), R, W)

    def tt(self, eng, out, in0, in1, op, R, W):
        self.S.op(eng, lambda e: e.tensor_tensor(out=out, in0=in0, in1=in1, op=op), R, W)

    def ts(self, eng, out, in0, s1, s2, op0, op1, R, W):
        if s2 is None:
            self.S.op(eng, lambda e: e.tensor_scalar(out=out, in0=in0, scalar1=s1, scalar2=None, op0=op0), R, W)
        else:
            self.S.op(eng, lambda e: e.tensor_scalar(out=out, in0=in0, scalar1=s1, scalar2=s2, op0=op0, op1=op1), R, W)

    def stt(self, out, in0, scalar, in1, op0, op1, R, W):
        self.S.op("dve", lambda e: e.scalar_tensor_tensor(out=out, in0=in0, scalar=scalar, in1=in1, op0=op0, op1=op1), R, W)

    def ttr(self, out, in0, in1, scale, accum, R, W):
        self.S.op("dve", lambda e: e.scalar_tensor_tensor(out=out, in0=in0, scalar=scale, in1=in1,
                                                          op0=ALU.mult, op1=ALU.mult, accum_out=accum), R, W)

    def recip(self, out, in_, R, W):
        self.S.op("dve", lambda e: e.reciprocal(out=out, in_=in_), R, W)

    def cp(self, eng, out, in_, R, W):
        if eng == "act":
            self.S.op("act", lambda e: e.copy(out=out, in_=in_), R, W)
        else:
            self.S.op(eng, lambda e: e.tensor_copy(out=out, in_=in_), R, W)

    def dma(self, eng, out, in_, R, W, final=False):
        self.S.dma(eng, out, in_, R, W, final=final)


CFG = dict(D=2048, S=16384, DFF=5632, NMEM=256)


def cfg_derive(cfg):
    c = dict(cfg)
    c["NK"] = c["D"] // 128
    c["NIN"] = 3 * 1024 + 3488 + 1024 + 3 * c["D"]
    return c


P_QG, P_KG, P_SLOPE, P_LI, P_SG, P_OML, P_LAM0 = 0, 1, 2, 3, 4, 5, 6
P_MU0, P_MU1 = 10, 17
P_W0, P_A0 = 24, 26
P_KK, P_KA, P_RK, P_LG, P_LB = 28, 29, 30, 31, 32
NPRM = 33
NCST = 322
NLORA = 512
NALB = 2688
EPS = 1e-6


def build_A(cfg, phases=("gemm", "attn", "prep", "scan", "post")):
    from contextlib import ExitStack
    c = cfg_derive(cfg)
    D, SQ, NK = c["D"], c["S"], c["NK"]
    TB = 128
    NTB = SQ // TB
    nc = bass.Bass("TRN2", target_bir_lowering=False)
    with ExitStack() as stack:
        C = Ctx(nc, stack)
        S = C.S
        S.setup(stack)
        h = H(C)
        xT = C.din("xT", [D, SQ])
        g = C.din("g", [128, NK])
        Wc = C.din("Wc", [D, 1280])
        cst = C.din("cst", [128, NCST])
        alb = C.din("alb", [128, NALB])
        prm = C.din("prm", [128, NPRM])
        lora = C.din("lora", [128, NLORA])
        od = C.dout("od", [128, SQ])
        orw = C.dout("orw", [128, SQ])
        prw = nc.dram_tensor("prw", [7 * 128, SQ], F32, kind="Internal").ap()
        rw = nc.dram_tensor("rw", [9 * 128, SQ], F32, kind="Internal").ap()
        gb = nc.dram_tensor("gb", [2 * 128, SQ], F32, kind="Internal").ap()
        yd = nc.dram_tensor("yd", [2 * 128, SQ], F32, kind="Internal").ap()
        prw_t, rw_t, gb_t, yd_t = S.tok(), S.tok(), S.tok(), S.tok()
        xT_v = xT.rearrange("(kc p) t -> p kc t", p=128)
        Wc_v = Wc.rearrange("(kc p) n -> p kc n", p=128)

        cs = C.sb([128, NCST], F32, "cs")
        cs_t = S.tok()
        h.dma("sp", cs[:, :], cst[:, :], [], [cs_t])
        ident = cs[:, 0:128]
        bd64 = cs[:, 128:256]
        stackI = cs[:, 256:320]
        hmask = cs[:, 320:322]
        pr = C.sb([128, NPRM + 8], F32, "pr")
        pr_t = S.tok()
        h.dma("sp", pr[:, 0:NPRM], prm[:, :], [], [pr_t])
        X_NW0, X_OMKA, X_NLAM = NPRM, NPRM + 2, NPRM + 3
        h.ts("dve", pr[:, X_NW0:X_NW0 + 2], pr[:, P_W0:P_W0 + 2], -1.0, None, ALU.mult, None, [pr_t], [pr_t])
        h.ts("dve", pr[:, X_OMKA:X_OMKA + 1], pr[:, P_KA:P_KA + 1], -1.0, 1.0, ALU.mult, ALU.add, [pr_t], [pr_t])
        ones_f = C.sb([128, 128], F32, "ones_f")
        ones_b = C.sb([128, 128], BF16, "ones_b")
        on_t = S.tok()
        S.op("pool", lambda e: e.memset(ones_f[:, :], 1.0), writes=[on_t])
        S.op("pool", lambda e: e.memset(ones_b[:, :], 1.0), writes=[on_t])
        tmps = [(C.sb([128, 512], F32, "tmp%d" % i), S.tok()) for i in range(6)]
        tmp_pool = RR(tmps)
        c_eps, c_eps_t = const_col(C, EPS)
        c_eps64, c_eps64_t = const_col(C, 64 * EPS)
        c_one, c_one_t = const_col(C, 1.0)
        c_mhalf, c_mhalf_t = const_col(C, -0.5)
        c_lnx, c_lnx_t = const_col(C, 64e-5)
        c_zero, c_zero_t = const_col(C, 0.0)
        pss = [(C.ps([128, 512], F32, "psA%d" % i), S.tok()) for i in range(8)]

        lt, lt_t = tmp_pool.next()
        h.tt("dve", lt[:, 0:1], pr[:, P_LAM0:P_LAM0 + 1], pr[:, P_LAM0 + 1:P_LAM0 + 2], ALU.mult, [pr_t], [lt_t])
        h.tt("dve", lt[:, 1:2], pr[:, P_LAM0 + 2:P_LAM0 + 3], pr[:, P_LAM0 + 3:P_LAM0 + 4], ALU.mult, [pr_t, lt_t], [lt_t])
        h.mm(pss[0][0][:, 0:2], ones_f[:, :], lt[:, 0:2], True, True, [lt_t, on_t], [pss[0][1]])
        h.act(lt[:, 2:4], pss[0][0][:, 0:2], AF.Exp, [pss[0][1], lt_t], [lt_t])
        h.tt("dve", lt[:, 4:5], lt[:, 3:4], lt[:, 2:3], ALU.subtract, [lt_t], [lt_t])
        h.tt("dve", pr[:, X_NLAM:X_NLAM + 1], lt[:, 4:5], pr[:, P_LI:P_LI + 1], ALU.subtract, [lt_t, pr_t], [pr_t])
        S.flush()

        with ExitStack() as st_qkv:
            C.stack = st_qkv
            qT = C.sb([128, SQ], BF16, "qT")
            kT = C.sb([128, SQ], BF16, "kT")
            vtok = C.sb([128, SQ // 128, 128], BF16, "vtok")
            qT_t, kT_t, vt_t = S.tok(), S.tok(), S.tok()
            if "gemm" in phases:
                with ExitStack() as st1:
                    C.stack = st1
                    g_sb = C.sb([128, NK], F32, "g_sb")
                    g_t = S.tok()
                    h.dma("sp", g_sb[:, :], g[:, :], [], [g_t])
                    Wb = C.sb([128, NK, 1280], BF16, "Wb")
                    Wb_t = S.tok()
                    wst = [(C.sb([128, NK, 128], F32, "wst%d" % i), S.tok()) for i in range(2)]
                    for j in range(10):
                        wf, wft = wst[j % 2]
                        h.dma("sp" if j % 2 == 0 else "pool", wf[:, :, :], Wc_v[:, :, j * 128:(j + 1) * 128], [], [wft])
                        h.cp("act" if j % 2 == 0 else "pool", Wb[:, :, j * 128:(j + 1) * 128], wf[:, :, :], [wft], [Wb_t])
                    xb = [(C.sb([128, NK, TB], F32, "xb%d" % i), S.tok()) for i in range(2)]
                    hTs = [(C.sb([128, NK, TB], BF16, "hT%d" % i), S.tok()) for i in range(2)]
                    outs = [(C.sb([128, TB], F32, "ob%d" % i), S.tok()) for i in range(4)]
                    outp = RR(outs)
                    ps_sum, ps_sum_t = pss[7]
                    psg = RR(pss[0:3])
                    ps2 = RR(pss[3:5])
                    pstr = RR(pss[5:7])
                    for tb in range(NTB):
                        t0 = tb * TB
                        xbuf, xt = xb[tb % 2]
                        hT, hT_t = hTs[tb % 2]
                        hk = NK // 2
                        h.dma("sp", xbuf[:, 0:hk, :], xT_v[:, 0:hk, t0:t0 + TB], [], [xt])
                        h.dma("pool", xbuf[:, hk:NK, :], xT_v[:, hk:NK, t0:t0 + TB], [], [xt])
                        rmsnorm_fm(C, xbuf, xt, g_sb, g_t, lambda kc, hT=hT: hT[:, kc, :], hT_t, NK, TB, ones_f, on_t,
                                   ps_sum, ps_sum_t, tmp_pool, D, EPS)
                        for j in range(10):
                            psb, pst = psg.next()
                            for kc in range(NK):
                                h.mm(psb[:, :TB], Wb[:, kc, j * 128:(j + 1) * 128], hT[:, kc, :], kc == 0, kc == NK - 1,
                                     [Wb_t, hT_t], [pst])
                            if j < 2:
                                sq, sqt = tmp_pool.next()
                                h.act(sq[:, :TB], psb[:, :TB], AF.Square, [pst], [sqt])
                                p2, p2t = ps2.next()
                                h.mm(p2[:, :TB], bd64, sq[:, :TB], True, True, [sqt, cs_t], [p2t])
                                rs, rst = tmp_pool.next()
                                if j == 0:
                                    h.act(rs[:, :TB], p2[:, :TB], AF.Sqrt, [p2t, c_eps64_t], [rst], scale=1.0, bias=c_eps64[:, 0:1])
                                else:
                                    h.act(rs[:, :TB], p2[:, :TB], AF.Sqrt, [p2t, c_eps_t], [rst], scale=1.0 / 64, bias=c_eps[:, 0:1])
                                h.recip(rs[:, :TB], rs[:, :TB], [rst], [rst])
                                dst, dst_t = (qT, qT_t) if j == 0 else (kT, kT_t)
                                h.stt(dst[:, t0:t0 + TB], psb[:, :TB], pr[:, P_QG + j:P_QG + j + 1], rs[:, :TB], ALU.mult, ALU.mult,
                                      [pst, rst, pr_t], [dst_t])
                            elif j == 2:
                                vf, vft = tmp_pool.next()
                                h.cp("act", vf[:, :TB], psb[:, :TB], [pst], [vft])
                                for q in range(TB // 128):
                                    pt, ptt = pstr.next()
                                    h.tr(pt[:, 0:128], vf[:, q * 128:(q + 1) * 128], ident, [vft, cs_t], [ptt])
                                    h.cp("dve", vtok[:, t0 // 128 + q, :], pt[:, 0:128], [ptt], [vt_t])
                            else:
                                ob, obt = outp.next()
                                h.cp("act" if j % 2 == 0 else "dve", ob[:, :], psb[:, :TB], [pst], [obt])
                                h.dma("sp", prw[(j - 3) * 128:(j - 2) * 128, t0:t0 + TB], ob[:, :], [obt], [prw_t])
                    S.flush()
                C.stack = st_qkv
            if "attn" in phases:
                with ExitStack() as st2:
                    C.stack = st2
                    al = C.sb([128, NALB], F32, "al")
                    al_t = S.tok()
                    h.dma("sp", al[:, :], alb[:, :], [], [al_t])
                    nsl = C.sb([128, 1], F32, "nsl")
                    h.ts("dve", nsl[:, :], pr[:, P_SLOPE:P_SLOPE + 1], -1.0, None, ALU.mult, None, [pr_t], [al_t])
                    for q in range(0, NALB, 512):
                        w = min(512, NALB - q)
                        h.ts("dve" if (q // 512) % 2 == 0 else "pool", al[:, q:q + w], al[:, q:q + w], nsl[:, 0:1], None, ALU.mult, None,
                             [al_t], [al_t])
                    Bm = al[:, 0:512]
                    sbs = RR([(C.sb([128, 512], F32, "sbs%d" % i), S.tok()) for i in range(3)])
                    Ps = RR([(C.sb([128, 512], BF16, "Pb%d" % i), S.tok()) for i in range(3)])
                    psc = RR(pss[4:7])
                    NKT = SQ // 128
                    for qb in range(SQ // 512):
                        i0 = qb * 512
                        for kt in range(NKT):
                            j0 = kt * 128
                            for m in range(2):
                                sc, sct = psc.next()
                                h.mm(sc[:, :], kT[m * 64:(m + 1) * 64, j0:j0 + 128], qT[m * 64:(m + 1) * 64, i0:i0 + 512], True, True,
                                     [kT_t, qT_t], [sct])
                                sb_, sbt = sbs.next()
                                if i0 - j0 >= 128:
                                    n = (i0 - j0) // 128
                                    h.stt(sb_[:, :], sc[:, :], al[:, 2560 + n:2561 + n], Bm, ALU.add, ALU.add, [sct, al_t], [sbt])
                                elif j0 - i0 >= 512:
                                    n = (j0 - i0) // 128
                                    h.stt(sb_[:, :], sc[:, :], al[:, 2560 + n:2561 + n], Bm, ALU.add, ALU.subtract, [sct, al_t], [sbt])
                                else:
                                    dd = (j0 - i0) // 128
                                    h.tt("dve", sb_[:, :], sc[:, :], al[:, 512 * (dd + 1):512 * (dd + 2)], ALU.add, [sct, al_t], [sbt])
                                Pb, Pt = Ps.next()
                                h.act(Pb[:, :], sb_[:, :], AF.Exp, [sbt], [Pt])
                                h.mm(pss[m][0][:, :], vtok[:, kt, :], Pb[:, :], kt == 0, kt == NKT - 1, [vt_t, Pt], [pss[m][1]])
                                h.mm(pss[2 + m][0][:, :], ones_b[:, :], Pb[:, :], kt == 0, kt == NKT - 1, [on_t, Pt], [pss[2 + m][1]])
                        r0, r0t = tmp_pool.next()
                        h.recip(r0[:, :], pss[2][0][:, :], [pss[2][1]], [r0t])
                        h.tt("dve", r0[:, :], pss[0][0][:, :], r0[:, :], ALU.mult, [pss[0][1], r0t], [r0t])
                        r1, r1t = tmp_pool.next()
                        h.recip(r1[:, :], pss[3][0][:, :], [pss[3][1]], [r1t])
                        h.tt("dve", r1[:, :], pss[1][0][:, :], r1[:, :], ALU.mult, [pss[1][1], r1t], [r1t])
                        o_, ot = tmp_pool.next()
                        h.stt(o_[:, :], r1[:, :], pr[:, X_NLAM:X_NLAM + 1], r0[:, :], ALU.mult, ALU.add, [r0t, r1t, pr_t], [ot])
                        sq, sqt = tmp_pool.next()
                        h.act(sq[:, :], o_[:, :], AF.Square, [ot], [sqt])
                        h.mm(pss[7][0][:, :], ones_f[:, :], sq[:, :], True, True, [sqt, on_t], [pss[7][1]])
                        h.act(sq[:, :], pss[7][0][:, :], AF.Sqrt, [pss[7][1], sqt, c_eps_t], [sqt], scale=1.0 / 128, bias=c_eps[:, 0:1])
                        h.recip(sq[:, :], sq[:, :], [sqt], [sqt])
                        h.ts("dve", sq[:, :], sq[:, :], pr[:, P_OML:P_OML + 1], None, ALU.mult, None, [sqt, pr_t], [sqt])
                        h.stt(o_[:, :], o_[:, :], pr[:, P_SG:P_SG + 1], sq[:, :], ALU.mult, ALU.mult, [ot, sqt, pr_t], [ot])
                        h.dma("sp", od[:, i0:i0 + 512], o_[:, :], [ot], [])
                    S.flush()
                C.stack = st_qkv
        C.stack = stack
        build_A_rwkv(C, h, c, phases, dict(cs=cs, cs_t=cs_t, pr=pr, pr_t=pr_t, ones_f=ones_f, on_t=on_t, tmp_pool=tmp_pool,
                                          pss=pss, prw=prw, rw=rw, gb=gb, yd=yd, orw=orw, lora=lora,
                                          X_NW0=X_NW0, X_OMKA=X_OMKA, c_one=c_one, c_one_t=c_one_t, c_mhalf=c_mhalf,
                                          c_mhalf_t=c_mhalf_t, c_lnx=c_lnx, c_lnx_t=c_lnx_t, c_zero=c_zero, c_zero_t=c_zero_t,
                                          ident=ident, bd64=bd64, stackI=stackI, hmask=hmask))
        S.flush(last=True)
    return nc


def build_A_rwkv(C, h, c, phases, E):
    from contextlib import ExitStack
    S = C.S
    SQ = c["S"]
    TB = 256
    NTB = SQ // TB
    cs_t, pr, pr_t, tmp_pool, pss = E["cs_t"], E["pr"], E["pr_t"], E["tmp_pool"], E["pss"]
    prw, rw, gb, yd, orw = E["prw"], E["rw"], E["gb"], E["yd"], E["orw"]
    ident, bd64, stackI, hmask = E["ident"], E["bd64"], E["stackI"], E["hmask"]
    X_NW0, X_OMKA = E["X_NW0"], E["X_OMKA"]

    def col(i):
        return pr[:, i:i + 1]

    if "prep" in phases:
        with ExitStack() as st3:
            C.stack = st3
            lo = C.sb([128, NLORA], F32, "lo")
            lo_t = S.tok()
            h.dma("sp", lo[:, :], E["lora"][:, :], [], [lo_t])
            raws = [[(C.sb([128, TB + 2], F32, "raw%d_%d" % (j, i)), S.tok()) for i in range(2)] for j in range(7)]
            wk = RR([(C.sb([128, TB], F32, "wk%d" % i), S.tok()) for i in range(24)])
            psr = RR(pss[0:6])
            for tb in range(NTB):
                t0 = tb * TB
                xs = []
                for j in range(7):
                    raw, rt = raws[j][tb % 2]
                    lo_c = 1 if tb == 0 else 0
                    hi_c = TB + 1 if tb == NTB - 1 else TB + 2
                    if tb == 0:
                        S.op("pool", lambda e, raw=raw: e.memset(raw[:, 0:1], 0.0), writes=[rt])
                    if tb == NTB - 1:
                        S.op("pool", lambda e, raw=raw: e.memset(raw[:, TB + 1:TB + 2], 0.0), writes=[rt])
                    h.dma("sp" if j % 2 == 0 else "pool", raw[:, lo_c:hi_c], prw[j * 128:(j + 1) * 128, t0 - 1 + lo_c:t0 - 1 + hi_c], [], [rt])
                    d0, d0t = wk.next()
                    h.tt("dve", d0[:, :], raw[:, 0:TB], raw[:, 1:TB + 1], ALU.subtract, [rt], [d0t])
                    x1, x1t = wk.next()
                    h.stt(x1[:, :], d0[:, :], col(P_MU0 + j), raw[:, 1:TB + 1], ALU.mult, ALU.add, [d0t, rt, pr_t], [x1t])
                    h.tt("pool", d0[:, :], raw[:, 2:TB + 2], raw[:, 1:TB + 1], ALU.subtract, [rt, d0t], [d0t])
                    h.stt(x1[:, :], d0[:, :], col(P_MU1 + j), x1[:, :], ALU.mult, ALU.add, [d0t, x1t, pr_t], [x1t])
                    xs.append((x1, x1t))
                (r_, r_t), (k_, k_t), (v_, v_t), (lw_, lw_t), (la_, la_t), (lg0, lg0t), (lg1, lg1t) = xs
                kk, kkt = wk.next()
                h.ts("dve", kk[:, :], k_[:, :], col(P_KK), None, ALU.mult, None, [k_t, pr_t], [kkt])
                sq, sqt = wk.next()
                h.act(sq[:, :], kk[:, :], AF.Square, [kkt], [sqt])
                p0, p0t = psr.next()
                h.mm(p0[:, :TB], bd64, sq[:, :], True, True, [sqt, cs_t], [p0t])
                h.act(sq[:, :], p0[:, :TB], AF.Sqrt, [p0t, sqt], [sqt])
                h.ts("dve", sq[:, :], sq[:, :], 1e-12, None, ALU.max, None, [sqt], [sqt])
                h.recip(sq[:, :], sq[:, :], [sqt], [sqt])
                h.tt("dve", kk[:, :], kk[:, :], sq[:, :], ALU.mult, [kkt, sqt], [kkt])
                rk, rkt = wk.next()
                h.stt(rk[:, :], r_[:, :], col(P_RK), k_[:, :], ALU.mult, ALU.mult, [r_t, k_t, pr_t], [rkt])
                p1, p1t = psr.next()
                h.mm(p1[:, :TB], bd64, rk[:, :], True, True, [rkt, cs_t], [p1t])
                h.tt("dve", rk[:, :], p1[:, :TB], v_[:, :], ALU.mult, [p1t, v_t, rkt], [rkt])
                h.dma("sp", gb[128:256, t0:t0 + TB], rk[:, :], [rkt], [])
                h.act(lg0[:, :], lg0[:, :], AF.Sigmoid, [lg0t], [lg0t])
                h.act(lg1[0:32, :], lg1[0:32, :], AF.Sigmoid, [lg1t], [lg1t])
                p2, p2t = psr.next()
                h.mm(p2[:, :TB], lo[:, 256:384], lg0[:, :], True, False, [lo_t, lg0t], [p2t])
                h.mm(p2[:, :TB], lo[0:32, 384:512], lg1[0:32, :], False, True, [lo_t, lg1t], [p2t])
                gg, ggt = wk.next()
                h.cp("act", gg[:, :], p2[:, :TB], [p2t], [ggt])
                h.dma("pool", gb[0:128, t0:t0 + TB], gg[:, :], [ggt], [])
                h.dma("sp", rw[0:128, t0:t0 + TB], r_[:, :], [r_t], [])
                h.dma("pool", rw[128:256, t0:t0 + TB], v_[:, :], [v_t], [])
                h.dma("sp", rw[256:384, t0:t0 + TB], kk[:, :], [kkt], [])
                h.act(lw_[:, :], lw_[:, :], AF.Tanh, [lw_t], [lw_t])
                for d in range(2):
                    pw, pwt = psr.next()
                    h.mm(pw[:, :TB], lo[d * 64:(d + 1) * 64, 0:128], lw_[d * 64:(d + 1) * 64, :], True, True, [lo_t, lw_t], [pwt])
                    e1, e1t = wk.next()
                    h.act(e1[:, :], pw[:, :TB], AF.Exp, [pwt, pr_t], [e1t], scale=-1.0, bias=col(X_NW0 + d))
                    h.act(e1[:, :], e1[:, :], AF.Ln, [e1t, E["c_one_t"]], [e1t], bias=E["c_one"][:, 0:1])
                    h.act(e1[:, :], e1[:, :], AF.Exp, [e1t, E["c_mhalf_t"]], [e1t], scale=-1.0, bias=E["c_mhalf"][:, 0:1])
                    h.act(e1[:, :], e1[:, :], AF.Exp, [e1t], [e1t], scale=-1.0)
                    h.dma("sp", rw[(3 + 3 * d) * 128:(4 + 3 * d) * 128, t0:t0 + TB], e1[:, :], [e1t], [])
                    pa, pat = psr.next()
                    h.mm(pa[:, :TB], lo[d * 64:(d + 1) * 64, 128:256], la_[d * 64:(d + 1) * 64, :], True, True, [lo_t, la_t], [pat])
                    a_, a_t = wk.next()
                    h.act(a_[:, :], pa[:, :TB], AF.Sigmoid, [pat, pr_t], [a_t], bias=col(P_A0 + d))
                    kd, kdt = wk.next()
                    h.ts("dve", kd[:, :], a_[:, :], col(P_KA), col(X_OMKA), ALU.mult, ALU.add, [a_t, pr_t], [kdt])
                    h.tt("dve", kd[:, :], kd[:, :], k_[:, :], ALU.mult, [kdt, k_t], [kdt])
                    h.dma("pool", rw[(4 + 3 * d) * 128:(5 + 3 * d) * 128, t0:t0 + TB], kd[:, :], [kdt], [])
                    h.tt("pool", a_[:, :], a_[:, :], kk[:, :], ALU.mult, [a_t, kkt], [a_t])
                    h.dma("sp", rw[(5 + 3 * d) * 128:(6 + 3 * d) * 128, t0:t0 + TB], a_[:, :], [a_t], [])
            S.flush()
        C.stack = C_stack_root(C)

    if "scan" in phases:
        with ExitStack() as st4:
            C.stack = st4
            WN = 64
            NW = SQ // WN
            ident3 = ident.rearrange("p (h j) -> p h j", h=2)
            sel = C.sb([128, 64 * 128], F32, "sel")
            sel_t = S.tok()
            for j in range(64):
                h.cp("dve" if j % 2 == 0 else "pool", sel[:, j * 128:(j + 1) * 128].rearrange("p (h v) -> p h v", h=2),
                     ident3[:, :, j:j + 1].broadcast_to([128, 2, 64]), [cs_t], [sel_t])
            St = [C.sb([128, 64], F32, "St%d" % d) for d in range(2)]
            St_t = [S.tok(), S.tok()]
            junk = [C.sb([128, 64], F32, "junk%d" % d) for d in range(2)]
            junk_t = [S.tok(), S.tok()]
            sa = [C.sb([128, 1], F32, "sa%d" % d) for d in range(2)]
            sa_t = [S.tok(), S.tok()]
            for d in range(2):
                S.op("pool", lambda e, d=d: e.memset(St[d][:, :], 0.0), writes=[St_t[d]])
            win = [[dict(x=C.sb([128, 6, WN], F32, "wx%d_%d" % (d, i)), xt=S.tok(),
                         lh=C.sb([128, 5 * 2 * WN], F32, "wl%d_%d" % (d, i)), lt=S.tok(),
                         stk=C.sb([128, 5 * 64], F32, "wk%d_%d" % (d, i)), st=S.tok(),
                         y=C.sb([128, WN], F32, "wy%d_%d" % (d, i)), yt=S.tok()) for i in range(2)] for d in range(2)]
            psb_ = [RR(pss[0:3]), RR(pss[3:6])]
            pst_ = RR(pss[6:8])
            def oprows(d):
                return [2, 3 + 3 * d, 5 + 3 * d, 4 + 3 * d, 0, 1]

            def prep_window(d, wi):
                W = win[d][wi % 2]
                tw = (wi if d == 0 else NW - 1 - wi) * WN
                for x, row in enumerate(oprows(d)):
                    h.dma("sp" if x % 2 == 0 else "pool", W["x"][:, x, :], rw[row * 128:(row + 1) * 128, tw:tw + WN], [], [W["xt"]])
                for x in range(5):
                    h.tt("pool", W["lh"][:, x * 128:(x + 1) * 128].rearrange("p (h j) -> p h j", h=2),
                         W["x"][:, x:x + 1, :].broadcast_to([128, 2, WN]),
                         hmask.unsqueeze(2).broadcast_to([128, 2, WN]), ALU.mult, [W["xt"], cs_t], [W["lt"]])
                    pt, ptt = pst_.next()
                    h.mm(pt[:, 0:64], W["lh"][:, x * 128:(x + 1) * 128], stackI, True, True, [W["lt"], cs_t], [ptt])
                    h.cp("act", W["stk"][:, x * 64:(x + 1) * 64], pt[:, 0:64], [ptt], [W["st"]])

            for d in range(2):
                prep_window(d, 0)
            for wi in range(NW):
                for d in range(2):
                    if wi + 1 < NW:
                        prep_window(d, wi + 1)
                for jj in range(WN):
                    for d in range(2):
                        W = win[d][wi % 2]
                        j = jj if d == 0 else WN - 1 - jj
                        bc, bct = psb_[d].next()
                        h.mm(bc[:, 0:320], sel[:, j * 128:(j + 1) * 128], W["stk"][:, :], True, True,
                             [W["st"], sel_t], [bct])
                        s_ = St[d]
                        h.ttr(junk[d][:, :], s_[:, :], bc[:, 0:64], -1.0, sa[d][:, 0:1], [St_t[d], bct], [junk_t[d], sa_t[d]])
                        h.tt("dve", s_[:, :], s_[:, :], bc[:, 64:128], ALU.mult, [St_t[d], bct, junk_t[d]], [St_t[d]])
                        h.stt(s_[:, :], bc[:, 128:192], sa[d][:, 0:1], s_[:, :], ALU.mult, ALU.add, [bct, sa_t[d], St_t[d]], [St_t[d]])
                        h.stt(s_[:, :], bc[:, 192:256], W["x"][:, 5, j:j + 1], s_[:, :], ALU.mult, ALU.add, [bct, W["xt"], St_t[d]], [St_t[d]])
                        h.ttr(junk[d][:, :], s_[:, :], bc[:, 256:320], 1.0, W["y"][:, j:j + 1], [St_t[d], bct, junk_t[d]],
                              [junk_t[d], W["yt"]])
                for d in range(2):
                    W = win[d][wi % 2]
                    tw = (wi if d == 0 else NW - 1 - wi) * WN
                    h.dma("sp", yd[d * 128:(d + 1) * 128, tw:tw + WN], W["y"][:, :], [W["yt"]], [])
            S.flush()
        C.stack = C_stack_root(C)

    if "post" in phases:
        with ExitStack() as st5:
            C.stack = st5
            wk = RR([(C.sb([128, TB], F32, "pk%d" % i), S.tok()) for i in range(12)])
            psr = RR(pss[0:4])
            for tb in range(NTB):
                t0 = tb * TB
                yf, yft = wk.next()
                yb, ybt = wk.next()
                bo, bot = wk.next()
                gg, ggt = wk.next()
                h.dma("sp", yf[:, :], yd[0:128, t0:t0 + TB], [], [yft])
                h.dma("pool", yb[:, :], yd[128:256, t0:t0 + TB], [], [ybt])
                h.dma("sp", bo[:, :], gb[128:256, t0:t0 + TB], [], [bot])
                h.dma("pool", gg[:, :], gb[0:128, t0:t0 + TB], [], [ggt])
                h.tt("dve", yf[:, :], yf[:, :], yb[:, :], ALU.add, [yft, ybt], [yft])
                pm, pmt = psr.next()
                h.mm(pm[:, :TB], bd64, yf[:, :], True, True, [yft, cs_t], [pmt])
                h.stt(yb[:, :], pm[:, :TB], -1.0 / 64, yf[:, :], ALU.mult, ALU.add, [pmt, yft, ybt], [ybt])
                h.act(yf[:, :], yb[:, :], AF.Square, [ybt, yft], [yft])
                pv, pvt = psr.next()
                h.mm(pv[:, :TB], bd64, yf[:, :], True, True, [yft, cs_t], [pvt])
                h.act(yf[:, :], pv[:, :TB], AF.Sqrt, [pvt, yft, E["c_lnx_t"]], [yft], scale=1.0 / 64, bias=E["c_lnx"][:, 0:1])
                h.recip(yf[:, :], yf[:, :], [yft], [yft])
                h.stt(yb[:, :], yb[:, :], col(P_LG), yf[:, :], ALU.mult, ALU.mult, [ybt, yft, pr_t], [ybt])
                h.stt(yb[:, :], yb[:, :], col(P_LB), bo[:, :], ALU.add, ALU.add, [ybt, bot, pr_t], [ybt])
                h.tt("dve", yb[:, :], yb[:, :], gg[:, :], ALU.mult, [ybt, ggt], [ybt])
                h.dma("sp", orw[:, t0:t0 + TB], yb[:, :], [ybt], [])
            S.flush()
        C.stack = C_stack_root(C)


def C_stack_root(C):
    return C.root_stack


def host_consts():
    cst = np.zeros((128, NCST), np.float32)
    cst[:, 0:128] = np.eye(128, dtype=np.float32)
    cst[0:64, 128:192] = 1.0
    cst[64:128, 192:256] = 1.0
    cst[0:64, 256:320] = np.eye(64, dtype=np.float32)
    cst[64:128, 256:320] = np.eye(64, dtype=np.float32)
    cst[0:64, 320] = 1.0
    cst[64:128, 321] = 1.0
    alb = np.zeros((128, NALB), np.float32)
    jj = np.arange(128, dtype=np.float32)[:, None]
    ii = np.arange(512, dtype=np.float32)[None, :]
    alb[:, 0:512] = ii - jj
    for d in range(4):
        alb[:, 512 * (d + 1):512 * (d + 2)] = np.abs(ii - jj - 128.0 * d)
    alb[:, 2560:2688] = 128.0 * np.arange(128, dtype=np.float32)[None, :]
    return cst, alb


def host_A_inputs(inp, l, xT, cfg):
    c = cfg_derive(cfg)
    D, NK = c["D"], c["NK"]
    W = inp["w_in"][l]
    cst, alb = host_consts()
    g = np.ascontiguousarray(inp["attn_norm_g"][l].reshape(NK, 128).T)
    lam_init = np.float32(0.8 - 0.6 * math.exp(-0.3 * l))
    RB = 3072
    maps = []
    for core in range(NCORES):
        ch = slice(core * 128, (core + 1) * 128)
        cols = np.concatenate([
            np.arange(core * 128, (core + 1) * 128), 1024 + np.arange(core * 128, (core + 1) * 128),
            2048 + np.arange(core * 128, (core + 1) * 128),
            RB + np.arange(core * 128, (core + 1) * 128), RB + 1024 + np.arange(core * 128, (core + 1) * 128),
            RB + 2048 + np.arange(core * 128, (core + 1) * 128),
            RB + 3072 + np.arange(128), RB + 3200 + np.arange(128), RB + 3328 + np.arange(160)])
        Wc = np.zeros((D, 1280), np.float32)
        Wc[:, :cols.size] = W[:, cols]
        prm = np.zeros((128, NPRM), np.float32)
        prm[:, P_QG] = inp["diff_qk_g"][l][0].reshape(128)
        prm[:, P_KG] = inp["diff_qk_g"][l][1].reshape(128)
        prm[:, P_SLOPE] = np.float32(2.0 ** (-(core + 1)))
        prm[:, P_LI] = lam_init
        prm[:, P_SG] = inp["diff_subln_g"][l]
        prm[:, P_OML] = np.float32(1.0) - lam_init
        prm[0:64, P_LAM0:P_LAM0 + 4] = inp["diff_lambda"][l].T
        mu = inp["rwkv_mu"][l]
        rcols = cols[384:] - RB
        for j in range(7):
            cc = rcols[j * 128:(j + 1) * 128]
            prm[:cc.size, P_MU0 + j] = mu[0][cc]
            prm[:cc.size, P_MU1 + j] = mu[1][cc]
        for d in range(2):
            prm[:, P_W0 + d] = inp["rwkv_w0"][l][d][ch]
            prm[:, P_A0 + d] = inp["rwkv_a0"][l][d][ch]
        prm[:, P_KK] = inp["rwkv_k_k"][l][ch]
        prm[:, P_KA] = inp["rwkv_k_a"][l][ch]
        prm[:, P_RK] = inp["rwkv_r_k"][l].reshape(1024)[ch]
        prm[:, P_LG] = inp["rwkv_lnx_g"][l][ch]
        prm[:, P_LB] = inp["rwkv_lnx_b"][l][ch]
        lora = np.zeros((128, NLORA), np.float32)
        for d in range(2):
            lora[d * 64:(d + 1) * 64, 0:128] = inp["rwkv_w2"][l][d][:, ch]
            lora[d * 64:(d + 1) * 64, 128:256] = inp["rwkv_a2"][l][d][:, ch]
        lora[:, 256:384] = inp["rwkv_g2"][l][0:128, ch]
        lora[0:32, 384:512] = inp["rwkv_g2"][l][128:160, ch]
        maps.append({"xT": xT, "g": g, "Wc": Wc, "cst": cst, "alb": alb, "prm": prm, "lora": lora})
    return maps


def chunks_of(n, maxw=512):
    assert n % 2 == 0 and maxw % 2 == 0
    half = n // 2
    k = (n + maxw - 1) // maxw
    out, s = [], 0
    for i in range(k):
        w = 2 * (half // k + (1 if i < half % k else 0))
        out.append((s, w))
        s += w
    assert s == n and all(w <= maxw for _, w in out)
    return out


def build_B(cfg):
    from contextlib import ExitStack
    c = cfg_derive(cfg)
    D, SQ, NK, DFF = c["D"], c["S"], c["NK"], c["DFF"]
    NF = DFF // 128
    TOKC = SQ // NCORES
    TOKB = min(512, TOKC)
    NSB = TOKC // TOKB
    SBW = TOKB + 2
    FB = min(256, TOKB)
    NMG = 1024 + 3 * D
    nc = bass.Bass("TRN2", target_bir_lowering=False)
    with ExitStack() as stack:
        C = Ctx(nc, stack)
        S = C.S
        S.setup(stack)
        h = H(C)
        xsT = C.din("xsT", [NSB * D, SBW])
        odT = C.din("odT", [NSB * 1024, SBW])
        orT = C.din("orT", [NSB * 1024, SBW])
        emk = C.din("emk", [NSB * 128, SBW])
        gvec = C.din("gvec", [128, 3 * NK])
        prmB = C.din("prmB", [128, 4 + 4 * NF])
        memT = C.din("memT", [D, 256])
        Wmg = C.din("Wmg", [D, NMG])
        Wkv = C.din("Wkv", [D, 2048])
        Wbr = C.din("Wbr", [3072, D])
        Wout = C.din("Wout", [D, D])
        Wup = C.din("Wup", [D, 2 * DFF])
        Wdn = C.din("Wdn", [DFF, D])
        x2T = C.dout("x2T", [D, TOKC])
        x1d = nc.dram_tensor("x1d", [D, SBW], F32, kind="Internal").ap()

        gv = C.sb([128, 3 * NK], F32, "gv")
        pb = C.sb([128, 4 + 4 * NF], F32, "pb")
        cn_t = S.tok()
        h.dma("sp", gv[:, :], gvec[:, :], [], [cn_t])
        h.dma("sp", pb[:, :], prmB[:, :], [], [cn_t])
        ones_f = C.sb([128, 128], F32, "ones_f")
        ones_b = C.sb([128, 128], BF16, "ones_b")
        S.op("pool", lambda e: e.memset(ones_f[:, :], 1.0), writes=[cn_t])
        S.op("pool", lambda e: e.memset(ones_b[:, :], 1.0), writes=[cn_t])
        c_eps, c_eps_t = const_col(C, EPS)
        c_eps256, c_eps256_t = const_col(C, 256 * EPS)
        pss = [(C.ps([128, 512], F32, "psB%d" % i), S.tok()) for i in range(8)]
        tmp_pool = RR([(C.sb([128, 512], F32, "tmpB%d" % i), S.tok()) for i in range(6)])
        wst = RR([(C.sb([128, 16, 128], F32, "wstB%d" % i), S.tok()) for i in range(2)])
        wbf = RR([(C.sb([128, 16, 128], BF16, "wbfB%d" % i), S.tok()) for i in range(3)])
        cnt = [0]
        for _i in range(4):
            qn_bufs(C, S, _i)
        for _i in range(3):
            acc_bufs(C, S, _i)

        def gemm(W2d, KT, c0, cw, rhs_of, chunks, psl, krows=128):
            Wv = W2d.rearrange("(kt p) n -> p kt n", p=128)
            for g0 in range(0, KT, 16):
                n = min(16, KT - g0)
                wf, wft = wst.next()
                wb, wbt = wbf.next()
                cnt[0] += 1
                h.dma("sp" if cnt[0] % 2 == 0 else "pool", wf[:, :n, :cw], Wv[:, g0:g0 + n, c0:c0 + cw], [], [wft])
                h.cp("act" if cnt[0] % 2 == 0 else "pool", wb[:, :n, :cw], wf[:, :n, :cw], [wft], [wbt])
                for ci, (cc0, w) in enumerate(chunks):
                    ps, pst = psl[ci]
                    for kt in range(n):
                        r, rt = rhs_of(g0 + kt, ci)
                        h.mm(ps[:cw, :w], wb[:, kt, :cw], r, g0 + kt == 0, g0 + kt == KT - 1, [wbt, rt], [pst])

        kmT = C.sb([128, 8, 256], BF16, "kmT")
        vm = C.sb([128, 2, 1024], BF16, "vm")
        km_t, vm_t = S.tok(), S.tok()
        with ExitStack() as st0:
            C.stack = st0
            mx = C.sb([128, NK, 256], F32, "mx")
            mn = C.sb([128, NK, 256], BF16, "mn")
            mx_t, mn_t = S.tok(), S.tok()
            h.dma("sp", mx[:, :, :], memT.rearrange("(kc p) t -> p kc t", p=128), [], [mx_t])
            rmsnorm_fm(C, mx, mx_t, gv[:, 2 * NK:3 * NK], cn_t, lambda kc: mn[:, kc, :], mn_t, NK, 256, ones_f, cn_t,
                       pss[7][0], pss[7][1], tmp_pool, D, EPS)
            ch = [(0, 256)]
            kraw = C.sb([128, 8, 256], F32, "kraw")
            kraw_t = S.tok()
            for j in range(8):
                gemm(Wkv, NK, j * 128, 128, lambda kt, ci: (mn[:, kt, :], mn_t), ch, [pss[j % 2]])
                h.cp("act", kraw[:, j, :], pss[j % 2][0][:, 0:256], [pss[j % 2][1]], [kraw_t])
            for hh in range(4):
                for j in range(2):
                    sq, sqt = tmp_pool.next()
                    h.act(sq[:, 0:256], kraw[:, hh * 2 + j, :], AF.Square, [kraw_t], [sqt])
                    h.mm(pss[2][0][:, 0:256], ones_f[:, :], sq[:, 0:256], j == 0, j == 1, [sqt, cn_t], [pss[2][1]])
                rs, rst = tmp_pool.next()
                h.act(rs[:, 0:256], pss[2][0][:, 0:256], AF.Sqrt, [pss[2][1], c_eps_t], [rst], scale=1.0 / 256, bias=c_eps[:, 0:1])
                h.recip(rs[:, 0:256], rs[:, 0:256], [rst], [rst])
                for j in range(2):
                    h.stt(kmT[:, hh * 2 + j, :], kraw[:, hh * 2 + j, :], pb[:, 2 + j:3 + j], rs[:, 0:256], ALU.mult, ALU.mult,
                          [kraw_t, rst, cn_t], [km_t])
            Wkv_v = Wkv.rearrange("(kt p) n -> p kt n", p=128)
            for cb in range(8):
                wf, wft = wst.next()
                wb, wbt = wbf.next()
                h.dma("sp", wf[:, :NK, :], Wkv_v[:, :, 1024 + cb * 128:1024 + (cb + 1) * 128], [], [wft])
                h.cp("act", wb[:, :NK, :], wf[:, :NK, :], [wft], [wbt])
                for mt in range(2):
                    ps, pst = pss[3 + mt]
                    for kc in range(NK):
                        h.mm(ps[:, 0:128], mn[:, kc, mt * 128:(mt + 1) * 128], wb[:, kc, :], kc == 0, kc == NK - 1, [mn_t, wbt], [pst])
                    h.cp("dve", vm[:, mt, cb * 128:(cb + 1) * 128], ps[:, 0:128], [pst], [vm_t])
            S.flush()
        C.stack = stack

        CH = chunks_of(SBW)
        for sb in range(NSB):
            xs_v = xsT[sb * D:(sb + 1) * D, :].rearrange("(kc p) t -> p kc t", p=128)
            with ExitStack() as st1:
                C.stack = st1
                hT = C.sb([128, NK, SBW], BF16, "hT")
                oT = C.sb([128, 24, SBW], BF16, "oT")
                mT = C.sb([128, NK, SBW], BF16, "mT")
                hT_t, oT_t, mT_t = S.tok(), S.tok(), S.tok()
                xst = [(C.sb([128, NK, 128], F32, "xst%d" % i), S.tok()) for i in range(2)]
                for bi, (b0, bw) in enumerate(chunks_of(SBW, 128)):
                    xb, xbt = xst[bi % 2]
                    h.dma("sp" if bi % 2 == 0 else "pool", xb[:, :, :bw], xs_v[:, :, b0:b0 + bw], [], [xbt])
                    rmsnorm_fm(C, xb, xbt, gv[:, 0:NK], cn_t, lambda kc, b0=b0, bw=bw: hT[:, kc, b0:b0 + bw], hT_t, NK, bw,
                               ones_f, cn_t, pss[7][0], pss[7][1], tmp_pool, D, EPS)
                for b, src in enumerate((odT, orT)):
                    for j in range(8):
                        for (cc0, w) in CH:
                            tb_, tbt = tmp_pool.next()
                            h.dma("sp" if j % 2 == 0 else "pool", tb_[:, :w], src[sb * 1024 + j * 128:sb * 1024 + (j + 1) * 128, cc0:cc0 + w],
                                  [], [tbt])
                            h.cp("act" if j % 2 == 0 else "pool", oT[:, b * 8 + j, cc0:cc0 + w], tb_[:, :w], [tbt], [oT_t])
                for hh in range(4):
                    for ci, (cc0, w) in enumerate(CH):
                        qn = []
                        for j in range(2):
                            gemm(Wmg, NK, hh * 256 + j * 128, 128, lambda kt, _ci, cc0=cc0, w=w: (hT[:, kt, cc0:cc0 + w], hT_t),
                                 [(cc0, w)], [pss[j]])
                        for j in range(2):
                            sq, sqt = tmp_pool.next()
                            h.act(sq[:, :w], pss[j][0][:, :w], AF.Square, [pss[j][1]], [sqt])
                            h.mm(pss[2][0][:, :w], ones_f[:, :], sq[:, :w], j == 0, j == 1, [sqt, cn_t], [pss[2][1]])
                        rs, rst = tmp_pool.next()
                        h.act(rs[:, :w], pss[2][0][:, :w], AF.Sqrt, [pss[2][1], c_eps256_t], [rst], scale=1.0, bias=c_eps256[:, 0:1])
                        h.recip(rs[:, :w], rs[:, :w], [rst], [rst])
                        for j in range(2):
                            q_, q_t = (C.sb([128, 512], BF16, "qn%d_%d_%d_%d" % (sb, hh, ci, j)), S.tok()) if False else qn_bufs(C, S, j)
                            h.stt(q_[:, :w], pss[j][0][:, :w], pb[:, j:j + 1], rs[:, :w], ALU.mult, ALU.mult, [pss[j][1], rst, cn_t], [q_t])
                            qn.append((q_, q_t))
                        Pm = []
                        for mt in range(2):
                            ps, pst = pss[3 + mt]
                            for j in range(2):
                                h.mm(ps[:, :w], kmT[:, hh * 2 + j, mt * 128:(mt + 1) * 128], qn[j][0][:, :w], j == 0, j == 1,
                                     [km_t, qn[j][1]], [pst])
                            P_, P_t = qn_bufs(C, S, 2 + mt)
                            h.act(P_[:, :w], ps[:, :w], AF.Exp, [pst], [P_t])
                            Pm.append((P_, P_t))
                        for mt in range(2):
                            h.mm(pss[5][0][:, :w], ones_b[:, :], Pm[mt][0][:, :w], mt == 0, mt == 1, [cn_t, Pm[mt][1]], [pss[5][1]])
                        rc, rct = tmp_pool.next()
                        h.recip(rc[:, :w], pss[5][0][:, :w], [pss[5][1]], [rct])
                        for vt in range(2):
                            ps, pst = pss[6]
                            for mt in range(2):
                                h.mm(ps[:, :w], vm[:, mt, hh * 256 + vt * 128:hh * 256 + (vt + 1) * 128], Pm[mt][0][:, :w], mt == 0, mt == 1,
                                     [vm_t, Pm[mt][1]], [pst])
                            h.tt("dve", oT[:, 16 + hh * 2 + vt, cc0:cc0 + w], ps[:, :w], rc[:, :w], ALU.mult, [pst, rct], [oT_t])
                NCH = len(CH)
                for i in range(NK):
                    for b in range(3):
                        gemm(Wmg, NK, 1024 + b * D + i * 128, 128, lambda kt, ci: (hT[:, kt, CH[ci][0]:CH[ci][0] + CH[ci][1]], hT_t),
                             CH, pss[0:NCH])
                        gemm(Wbr[b * 1024:(b + 1) * 1024, :], 8, i * 128, 128,
                             lambda kt, ci, b=b: (oT[:, b * 8 + kt, CH[ci][0]:CH[ci][0] + CH[ci][1]], oT_t), CH, pss[4:4 + NCH])
                        for ci, (cc0, w) in enumerate(CH):
                            sg, sgt = tmp_pool.next()
                            h.act(sg[:, :w], pss[ci][0][:, :w], AF.Sigmoid, [pss[ci][1]], [sgt])
                            if b == 0:
                                ac, act_ = acc_bufs(C, S, ci)
                                h.tt("dve", ac[:, :w], sg[:, :w], pss[4 + ci][0][:, :w], ALU.mult, [sgt, pss[4 + ci][1]], [act_])
                            else:
                                ac, act_ = acc_bufs(C, S, ci)
                                h.tt("dve", sg[:, :w], sg[:, :w], pss[4 + ci][0][:, :w], ALU.mult, [sgt, pss[4 + ci][1]], [sgt])
                                if b == 1:
                                    h.tt("pool", ac[:, :w], ac[:, :w], sg[:, :w], ALU.add, [sgt, act_], [act_])
                                else:
                                    h.tt("dve", mT[:, i, cc0:cc0 + w], ac[:, :w], sg[:, :w], ALU.add, [sgt, act_], [mT_t])
                x1_t = S.tok()
                for i in range(NK):
                    gemm(Wout, NK, i * 128, 128, lambda kt, ci: (mT[:, kt, CH[ci][0]:CH[ci][0] + CH[ci][1]], mT_t), CH, pss[0:NCH])
                    for ci, (cc0, w) in enumerate(CH):
                        xr, xrt = tmp_pool.next()
                        h.dma("sp", xr[:, :w], xsT[sb * D + i * 128:sb * D + (i + 1) * 128, cc0:cc0 + w], [], [xrt])
                        h.tt("dve", xr[:, :w], xr[:, :w], pss[ci][0][:, :w], ALU.add, [xrt, pss[ci][1]], [xrt])
                        h.dma("pool", x1d[i * 128:(i + 1) * 128, cc0:cc0 + w], xr[:, :w], [xrt], [x1_t])
                S.flush()
            C.stack = stack
            with ExitStack() as st2:
                C.stack = st2
                h2 = C.sb([128, NK, SBW], BF16, "h2")
                h2_t = S.tok()
                xst = [(C.sb([128, NK, 128], F32, "xs2_%d" % i), S.tok()) for i in range(2)]
                x1v = x1d.rearrange("(kc p) t -> p kc t", p=128)
                for bi, (b0, bw) in enumerate(chunks_of(SBW, 128)):
                    xb, xbt = xst[bi % 2]
                    h.dma("sp" if bi % 2 == 0 else "pool", xb[:, :, :bw], x1v[:, :, b0:b0 + bw], [], [xbt])
                    rmsnorm_fm(C, xb, xbt, gv[:, NK:2 * NK], cn_t, lambda kc, b0=b0, bw=bw: h2[:, kc, b0:b0 + bw], h2_t, NK, bw,
                               ones_f, cn_t, pss[7][0], pss[7][1], tmp_pool, D, EPS)
                em = C.sb([128, SBW], F32, "em")
                em_t = S.tok()
                h.dma("sp", em[:, :], emk[sb * 128:(sb + 1) * 128, :], [], [em_t])
                aT = C.sb([128, NF, FB], BF16, "aT")
                aT_t = S.tok()
                for fb in range(TOKB // FB):
                    f0 = fb * FB
                    fw = FB + 2
                    for f in range(NF):
                        gemm(Wup, NK, f * 128, 128, lambda kt, ci, f0=f0, fw=fw: (h2[:, kt, f0:f0 + fw], h2_t), [(0, fw)], [pss[0 + 2 * (f % 2)]])
                        gemm(Wup, NK, DFF + f * 128, 128, lambda kt, ci, f0=f0, fw=fw: (h2[:, kt, f0:f0 + fw], h2_t), [(0, fw)],
                             [pss[1 + 2 * (f % 2)]])
                        pg, pgt = pss[0 + 2 * (f % 2)]
                        pv, pvt = pss[1 + 2 * (f % 2)]
                        ug, ugt = tmp_pool.next()
                        h.tt("dve", ug[:, :fw], pg[:, :fw], em[:, f0:f0 + fw], ALU.mult, [pgt, em_t], [ugt])
                        t1, t1t = tmp_pool.next()
                        cw = 4 + 4 * f
                        h.ts("dve", t1[:, :FB], ug[:, 0:FB], pb[:, cw:cw + 1], pb[:, cw + 3:cw + 4], ALU.mult, ALU.add, [ugt, cn_t], [t1t])
                        h.stt(t1[:, :FB], ug[:, 1:FB + 1], pb[:, cw + 1:cw + 2], t1[:, :FB], ALU.mult, ALU.add, [ugt, t1t, cn_t], [t1t])
                        h.stt(t1[:, :FB], ug[:, 2:FB + 2], pb[:, cw + 2:cw + 3], t1[:, :FB], ALU.mult, ALU.add, [ugt, t1t, cn_t], [t1t])
                        h.act(ug[:, :FB], t1[:, :FB], AF.Sigmoid, [t1t, ugt], [ugt])
                        h.tt("pool", t1[:, :FB], t1[:, :FB], ug[:, :FB], ALU.mult, [t1t, ugt], [t1t])
                        h.tt("dve", aT[:, f, :], t1[:, :FB], pv[:, 1:FB + 1], ALU.mult, [t1t, pvt], [aT_t])
                    for i in range(NK):
                        gemm(Wdn, NF, i * 128, 128, lambda kt, ci: (aT[:, kt, :], aT_t), [(0, FB)], [pss[4 + i % 2]])
                        xr, xrt = tmp_pool.next()
                        h.dma("sp", xr[:, :FB], x1d[i * 128:(i + 1) * 128, f0 + 1:f0 + 1 + FB], [], [xrt])
                        h.tt("dve", xr[:, :FB], xr[:, :FB], pss[4 + i % 2][0][:, :FB], ALU.add, [xrt, pss[4 + i % 2][1]], [xrt])
                        h.dma("pool", x2T[i * 128:(i + 1) * 128, sb * TOKB + f0:sb * TOKB + f0 + FB], xr[:, :FB], [xrt], [])
                S.flush()
            C.stack = stack
    return nc


_QB = {}


def qn_bufs(C, S, i):
    key = (id(C), "q", i)
    if key not in _QB:
        st = C.stack
        C.stack = C.root_stack
        _QB[key] = (C.sb([128, 512], BF16, "qnb%d" % i), S.tok())
        C.stack = st
    return _QB[key]


def acc_bufs(C, S, i):
    key = (id(C), "a", i)
    if key not in _QB:
        st = C.stack
        C.stack = C.root_stack
        _QB[key] = (C.sb([128, 512], F32, "accb%d" % i), S.tok())
        C.stack = st
    return _QB[key]


def host_B_inputs(inp, l, x_full, od_full, or_full, cfg):
    c = cfg_derive(cfg)
    D, SQ, NK, DFF = c["D"], c["S"], c["NK"], c["DFF"]
    NF = DFF // 128
    TOKC = SQ // NCORES
    TOKB = min(512, TOKC)
    NSB = TOKC // TOKB
    SBW = TOKB + 2
    W = inp["w_in"][l]
    Wmg = np.ascontiguousarray(W[:, 3072 + 3488:])
    gvec = np.concatenate([inp["attn_norm_g"][l].reshape(NK, 128).T, inp["ffn_norm_g"][l].reshape(NK, 128).T,
                           inp["mem_norm_g"][l].reshape(NK, 128).T], axis=1).astype(np.float32)
    prmB = np.zeros((128, 4 + 4 * NF), np.float32)
    prmB[:, 0:2] = inp["mem_qk_g"][l][0].reshape(2, 128).T
    prmB[:, 2:4] = inp["mem_qk_g"][l][1].reshape(2, 128).T
    for f in range(NF):
        for k in range(3):
            prmB[:, 4 + 4 * f + k] = inp["ffn_conv_w"][l][k][f * 128:(f + 1) * 128]
        prmB[:, 4 + 4 * f + 3] = inp["ffn_conv_b"][l][f * 128:(f + 1) * 128]
    memT = np.ascontiguousarray(inp["mem"][0].T)
    Wbr = np.ascontiguousarray(inp["w_branch"][l].reshape(3072, D))
    xp = np.zeros((SQ + 2, D), np.float32)
    xp[1:-1] = x_full
    odp = np.zeros((SQ + 2, 1024), np.float32)
    odp[1:-1] = od_full
    orp = np.zeros((SQ + 2, 1024), np.float32)
    orp[1:-1] = or_full
    valid = np.zeros((SQ + 2,), np.float32)
    valid[1:-1] = 1.0
    maps = []
    for core in range(NCORES):
        xs, ods, ors, ems = [], [], [], []
        for sb in range(NSB):
            a = core * TOKC + sb * TOKB
            xs.append(xp[a:a + SBW].T)
            ods.append(odp[a:a + SBW].T)
            ors.append(orp[a:a + SBW].T)
            ems.append(np.broadcast_to(valid[a:a + SBW][None, :], (128, SBW)))
        maps.append({"xsT": np.ascontiguousarray(np.concatenate(xs, 0)), "odT": np.ascontiguousarray(np.concatenate(ods, 0)),
                     "orT": np.ascontiguousarray(np.concatenate(ors, 0)), "emk": np.ascontiguousarray(np.concatenate(ems, 0)),
                     "gvec": gvec, "prmB": prmB, "memT": memT, "Wmg": Wmg, "Wkv": inp["w_mem_kv"][l], "Wbr": Wbr,
                     "Wout": inp["w_out"][l], "Wup": inp["w_ffn_up"][l], "Wdn": inp["w_ffn_down"][l]})
    return maps


_PROGS = {}


def _prog(key, fn):
    if key not in _PROGS:
        _PROGS[key] = fn()
    return _PROGS[key]


def build_warm(shapes):
    nc = bass.Bass("TRN2", target_bir_lowering=False)
    ins = [nc.dram_tensor("w%d" % i, list(s), F32, kind="ExternalInput").ap() for i, s in enumerate(shapes)]
    y = nc.dram_tensor("y", [128, 16], F32, kind="ExternalOutput").ap()
    with nc.sbuf_tensor("wt", [128, 16], F32) as t, nc.semaphore("a") as a, nc.semaphore("b") as b:
        n = 0
        for ap in ins:
            rows = min(128, ap.shape[0])
            cols = min(16, ap.shape[1])
            nc.sync.dma_start(out=t[:rows, :cols], in_=ap[0:rows, 0:cols]).then_inc(a, 16)
            n += 16
            nc.sync.wait_ge(a, n)
        nc.sync.dma_start(out=y[:, :], in_=t[:, :]).then_inc(b, 16)
        nc.sync.wait_ge(b, 16)
    return nc


def _warm(arrays):
    arrays = [np.ascontiguousarray(a, dtype=np.float32) for a in arrays]
    shapes = tuple(a.shape for a in arrays)
    nc = _prog(("warm", shapes), lambda: build_warm(shapes))
    maps = []
    for core in range(NCORES):
        maps.append({"w%d" % i: (a if core == 0 else np.zeros_like(a)) for i, a in enumerate(arrays)})
    run_bass_kernel_spmd(nc, maps, core_ids=list(range(NCORES)))


def kernel(**inp):
    inp = {k: np.asarray(v) for k, v in inp.items()}
    x = np.ascontiguousarray(inp["x"][0], dtype=np.float32)
    SQ, D = x.shape
    cfg = dict(D=D, S=SQ, DFF=inp["ffn_conv_b"].shape[1], NMEM=inp["mem"].shape[1])
    key = (D, SQ, cfg["DFF"])
    ncA = _prog(("A",) + key, lambda: build_A(cfg))
    ncB = _prog(("B",) + key, lambda: build_B(cfg))
    depth = inp["w_in"].shape[0]
    big = x.size * 4 > (8 << 20)
    for l in range(depth):
        xT = np.ascontiguousarray(x.T)
        if big:
            _warm([xT])
        mapsA = host_A_inputs(inp, l, xT, cfg)
        resA = run_bass_kernel_spmd(ncA, mapsA, core_ids=list(range(NCORES)))
        od = np.concatenate([resA.results[c]["od"].T for c in range(NCORES)], axis=1)
        orw = np.concatenate([resA.results[c]["orw"].T for c in range(NCORES)], axis=1)
        del resA, mapsA
        mapsB = host_B_inputs(inp, l, x, od, orw, cfg)
        if big:
            m0 = mapsB[0]
            _warm([m0["Wmg"], m0["Wkv"], m0["Wbr"], m0["Wout"], m0["Wup"], m0["Wdn"]])
        resB = run_bass_kernel_spmd(ncB, mapsB, core_ids=list(range(NCORES)))
        x = np.ascontiguousarray(np.concatenate([resB.results[c]["x2T"].T for c in range(NCORES)], axis=0), dtype=np.float32)
        del resB, mapsB
    return x[None].astype(np.float32)
```
